# Optimizing a Trainium2 kernel written in Bass

```python
import math
import jax, jax.numpy as jnp
from jax import lax
import numpy as np

D_MODEL = 1024
BATCH = 2
SEQ = 8192
DEPTH = 1
DEC_BATCH = 32
DEC_SEQ = 2048
PAST_LEN = 128

GRID_W = 64
Q_BLOCK = 128
A_HEADS = 8
A_QK_DIM = 64
A_V_DIM = 2 * A_QK_DIM
B_HEADS = 8
B_KV_HEADS = 2
B_HEAD_DIM = 128
B_GROUP = B_HEADS // B_KV_HEADS
ROPE_THETA = 10000.0
ROPE_AXIS_DIM = B_HEAD_DIM // 2
REL_BUCKETS = 32
REL_MAX_DIST = 128
D_FF = -(-8 * D_MODEL // (3 * 256)) * 256
LN_EPS = 1e-5
RMS_EPS = 1e-6
DEEPNORM_ALPHA = (2 * DEPTH) ** 0.25
DEEPNORM_BETA = (8 * DEPTH) ** -0.25

A_Q_W = A_HEADS * 2 * A_QK_DIM
A_K_W = A_HEADS * 2 * A_QK_DIM
A_V_W = A_HEADS * A_V_DIM
B_Q_W = B_HEADS * B_HEAD_DIM
B_KV_W = B_KV_HEADS * B_HEAD_DIM
GATE_W = D_MODEL
IN_WIDTHS = [A_Q_W, A_K_W, A_V_W, B_Q_W, B_KV_W, B_KV_W, GATE_W, GATE_W]
IN_TOTAL = sum(IN_WIDTHS)
IN_SPLITS = [int(v) for v in np.cumsum(IN_WIDTHS)[:-1]]

kernel_name = "hybrid_diffattn_axialgqa_deepnorm_encoder"


def layer_norm(x, g, b):
    xf = x.astype(jnp.float32)
    mu = jnp.mean(xf, axis=-1, keepdims=True)
    var = jnp.mean(jnp.square(xf - mu), axis=-1, keepdims=True)
    return ((xf - mu) * lax.rsqrt(var + LN_EPS) * g.astype(jnp.float32) + b.astype(jnp.float32)).astype(x.dtype)


def rms_norm(x, g):
    xf = x.astype(jnp.float32)
    ms = jnp.mean(jnp.square(xf), axis=-1, keepdims=True)
    return (xf * lax.rsqrt(ms + RMS_EPS) * g.astype(jnp.float32)).astype(x.dtype)


def t5_bucket(rel):
    nb = REL_BUCKETS // 2
    max_exact = nb // 2
    ret = (rel > 0).astype(jnp.int32) * nb
    n = jnp.abs(rel)
    nf = jnp.maximum(n, 1).astype(jnp.float32)
    large = max_exact + (jnp.log(nf / max_exact) / math.log(REL_MAX_DIST / max_exact)
                         * (nb - max_exact)).astype(jnp.int32)
    large = jnp.minimum(large, nb - 1)
    return ret + jnp.where(n < max_exact, n, large)


def axial_rope_tables(S):
    rows = S // GRID_W
    row = jnp.repeat(jnp.arange(rows), GRID_W).astype(jnp.float32)
    col = jnp.tile(jnp.arange(GRID_W), rows).astype(jnp.float32)
    freqs = ROPE_THETA ** (-jnp.arange(0, ROPE_AXIS_DIM, 2, dtype=jnp.float32) / ROPE_AXIS_DIM)
    ang_r = row[:, None] * freqs[None, :]
    ang_c = col[:, None] * freqs[None, :]
    cos = jnp.concatenate([jnp.cos(ang_r), jnp.cos(ang_r), jnp.cos(ang_c), jnp.cos(ang_c)], axis=-1)
    sin = jnp.concatenate([jnp.sin(ang_r), jnp.sin(ang_r), jnp.sin(ang_c), jnp.sin(ang_c)], axis=-1)
    return cos, sin


def apply_axial_rope(x, cos, sin):
    half = ROPE_AXIS_DIM // 2
    xs = x.reshape(x.shape[:-1] + (2, 2, half))
    rot = jnp.stack([-xs[..., 1, :], xs[..., 0, :]], axis=-2).reshape(x.shape)
    c = cos[None, :, None, :].astype(x.dtype)
    s = sin[None, :, None, :].astype(x.dtype)
    return x * c + rot * s


def diff_attention(qa, ka, va, lam, lam_init, subln_g, rel_bias):
    B, S, _ = qa.shape
    nblk = S // Q_BLOCK
    scale = A_QK_DIM ** -0.5
    q = qa.reshape(B, nblk, Q_BLOCK, A_HEADS, 2, A_QK_DIM).transpose(1, 0, 2, 3, 4, 5)
    k = ka.reshape(B, S, A_HEADS, 2, A_QK_DIM)
    v = va.reshape(B, S, A_HEADS, A_V_DIM)
    kpos = jnp.arange(S, dtype=jnp.int32)

    def block(args):
        qb, start = args
        qpos = start + jnp.arange(Q_BLOCK, dtype=jnp.int32)
        bias = rel_bias[t5_bucket(kpos[None, :] - qpos[:, None])]
        bias = bias.transpose(2, 0, 1).astype(jnp.float32)
        s = jnp.einsum('bqhcd,bkhcd->cbhqk', qb, k).astype(jnp.float32) * scale + bias
        p = jax.nn.softmax(s, axis=-1)
        a = p[0] - lam * p[1]
        return jnp.einsum('bhqk,bkhe->bqhe', a.astype(v.dtype), v)

    starts = jnp.arange(nblk, dtype=jnp.int32) * Q_BLOCK
    o = lax.map(block, (q, starts))
    o = o.transpose(1, 0, 2, 3, 4).reshape(B, S, A_HEADS, A_V_DIM)
    o = rms_norm(o, subln_g) * (1.0 - lam_init)
    return o.reshape(B, S, A_V_W)


def axial_gqa(qb_, kb_, vb_, q_norm_g, k_norm_g):
    B, S, _ = qb_.shape
    nblk = S // Q_BLOCK
    scale = B_HEAD_DIM ** -0.5
    cos, sin = axial_rope_tables(S)
    q = apply_axial_rope(rms_norm(qb_.reshape(B, S, B_HEADS, B_HEAD_DIM), q_norm_g), cos, sin)
    k = apply_axial_rope(rms_norm(kb_.reshape(B, S, B_KV_HEADS, B_HEAD_DIM), k_norm_g), cos, sin)
    v = vb_.reshape(B, S, B_KV_HEADS, B_HEAD_DIM)
    q = q.reshape(B, nblk, Q_BLOCK, B_KV_HEADS, B_GROUP, B_HEAD_DIM).transpose(1, 0, 2, 3, 4, 5)

    def block(qblk):
        s = jnp.einsum('bqkgd,bskd->bkgqs', qblk, k).astype(jnp.float32) * scale
        p = jax.nn.softmax(s, axis=-1)
        return jnp.einsum('bkgqs,bskd->bqkgd', p.astype(v.dtype), v)

    o = lax.map(block, q)
    return o.transpose(1, 0, 2, 3, 4, 5).reshape(B, S, B_Q_W)


def encoder_layer(x, layer_idx, w_in, lambda_q1, lambda_k1, lambda_q2, lambda_k2, subln_g,
                  q_norm_g, k_norm_g, rel_bias, w_proj_a, w_proj_b, w_o, ln1_g, ln1_b,
                  w_gate, w_up, w_down, ln2_g, ln2_b):
    lam_init = 0.8 - 0.6 * math.exp(-0.3 * layer_idx)
    lam = (jnp.exp(jnp.sum(lambda_q1.astype(jnp.float32) * lambda_k1.astype(jnp.float32)))
           - jnp.exp(jnp.sum(lambda_q2.astype(jnp.float32) * lambda_k2.astype(jnp.float32)))
           + lam_init)
    proj = x @ w_in
    qa, ka, va, qb, kb, vb, ga, gb = jnp.split(proj, IN_SPLITS, axis=-1)
    oa = diff_attention(qa, ka, va, lam, lam_init, subln_g, rel_bias)
    ob = axial_gqa(qb, kb, vb, q_norm_g, k_norm_g)
    merged = jax.nn.sigmoid(ga) * (oa @ w_proj_a) + jax.nn.sigmoid(gb) * (ob @ w_proj_b)
    h = layer_norm(DEEPNORM_ALPHA * x + merged @ w_o, ln1_g, ln1_b)
    ffn = (jax.nn.silu(h @ w_gate) * (h @ w_up)) @ w_down
    return layer_norm(DEEPNORM_ALPHA * h + ffn, ln2_g, ln2_b)


def setup_inputs(seed: int = 0) -> dict:
    key = jax.random.key(seed)
    ks = jax.random.split(key, 32)
    f32 = jnp.float32

    def nrm(k, shape, scale):
        return jax.random.normal(k, shape, f32) * scale

    s_in = D_MODEL ** -0.5
    beta = DEEPNORM_BETA
    cols = [
        nrm(ks[2], (DEPTH, D_MODEL, A_Q_W), s_in),
        nrm(ks[3], (DEPTH, D_MODEL, A_K_W), s_in),
        nrm(ks[4], (DEPTH, D_MODEL, A_V_W), s_in * beta),
        nrm(ks[5], (DEPTH, D_MODEL, B_Q_W), s_in),
        nrm(ks[6], (DEPTH, D_MODEL, B_KV_W), s_in),
        nrm(ks[7], (DEPTH, D_MODEL, B_KV_W), s_in * beta),
        nrm(ks[8], (DEPTH, D_MODEL, GATE_W), s_in),
        nrm(ks[9], (DEPTH, D_MODEL, GATE_W), s_in),
    ]
    return {
        "x_prompt": jax.random.normal(ks[0], (BATCH, SEQ, D_MODEL), f32),
        "x_sample": jax.random.normal(ks[1], (DEC_BATCH, DEC_SEQ, D_MODEL), f32),
        "w_in": jnp.concatenate(cols, axis=-1),
        "lambda_q1": nrm(ks[10], (DEPTH, A_QK_DIM), 0.1),
        "lambda_k1": nrm(ks[11], (DEPTH, A_QK_DIM), 0.1),
        "lambda_q2": nrm(ks[12], (DEPTH, A_QK_DIM), 0.1),
        "lambda_k2": nrm(ks[13], (DEPTH, A_QK_DIM), 0.1),
        "subln_g": 1.0 + nrm(ks[14], (DEPTH, A_V_DIM), 0.02),
        "q_norm_g": 1.0 + nrm(ks[15], (DEPTH, B_HEAD_DIM), 0.02),
        "k_norm_g": 1.0 + nrm(ks[16], (DEPTH, B_HEAD_DIM), 0.02),
        "rel_bias": nrm(ks[17], (REL_BUCKETS, A_HEADS), 0.5),
        "w_proj_a": nrm(ks[18], (DEPTH, A_V_W, D_MODEL), A_V_W ** -0.5 * beta),
        "w_proj_b": nrm(ks[19], (DEPTH, B_Q_W, D_MODEL), B_Q_W ** -0.5 * beta),
        "w_o": nrm(ks[20], (DEPTH, D_MODEL, D_MODEL), s_in * beta),
        "ln1_g": 1.0 + nrm(ks[21], (DEPTH, D_MODEL), 0.02),
        "ln1_b": nrm(ks[22], (DEPTH, D_MODEL), 0.02),
        "w_gate": nrm(ks[23], (DEPTH, D_MODEL, D_FF), s_in),
        "w_up": nrm(ks[24], (DEPTH, D_MODEL, D_FF), s_in),
        "w_down": nrm(ks[25], (DEPTH, D_FF, D_MODEL), D_FF ** -0.5 * beta),
        "ln2_g": 1.0 + nrm(ks[26], (DEPTH, D_MODEL), 0.02),
        "ln2_b": nrm(ks[27], (DEPTH, D_MODEL), 0.02),
    }


def reference(x_prompt, x_sample, w_in, lambda_q1, lambda_k1, lambda_q2, lambda_k2, subln_g,
              q_norm_g, k_norm_g, rel_bias, w_proj_a, w_proj_b, w_o, ln1_g, ln1_b,
              w_gate, w_up, w_down, ln2_g, ln2_b):
    def run(x):
        for l in range(DEPTH):
            x = encoder_layer(x, l, w_in[l], lambda_q1[l], lambda_k1[l], lambda_q2[l], lambda_k2[l],
                              subln_g[l], q_norm_g[l], k_norm_g[l], rel_bias, w_proj_a[l],
                              w_proj_b[l], w_o[l], ln1_g[l], ln1_b[l], w_gate[l], w_up[l],
                              w_down[l], ln2_g[l], ln2_b[l])
        return x

    y_prompt = run(x_prompt)
    y_sample = run(x_sample)
    return (y_prompt, y_sample)
```

```python
import math
import os
from contextlib import ExitStack
import numpy as np
import concourse.bass as bass
import concourse.mybir as mybir
from concourse.bass_utils import run_bass_kernel_spmd

F32 = mybir.dt.float32
BF16 = mybir.dt.bfloat16
AF = mybir.ActivationFunctionType
ALU = mybir.AluOpType
AX = mybir.AxisListType

PE, ACT, DVE, POOL, SP = "pe", "act", "dve", "pool", "sp"
ENGS = [PE, ACT, DVE, POOL, SP]

D = 1024
KC = 8
NT = 512
DFF = 2816
NFC = 22
ALPHA = 2.0 ** 0.25
LAM_INIT = 0.8 - 0.6 * math.exp(0.0)
LN_EPS = 1e-5
RMS_EPS = 1e-6
GHOST = -30000.0
OFF_QA, OFF_KA, OFF_VA, OFF_QB, OFF_KB, OFF_VB, OFF_GA, OFF_GB = 0, 1024, 2048, 3072, 4096, 4352, 4608, 5632
NSLOT_P = 65
NSLOT_PP = 68
EPOCH = 4000
PAIR = True
VW = 136


class Buf:
    __slots__ = ("name", "writers", "readers", "ndma", "excl")

    def __init__(self, name, excl=False):
        self.name = name
        self.excl = excl
        self.writers = []
        self.readers = []
        self.ndma = 0


class Op:
    __slots__ = ("eng", "emit", "deps", "is_dma", "sembuf", "dma_cnt", "sig_idx", "signals", "seq")

    def __init__(self, eng, emit, is_dma, sembuf):
        self.eng = eng
        self.emit = emit
        self.deps = []
        self.is_dma = is_dma
        self.sembuf = sembuf
        self.dma_cnt = 0
        self.sig_idx = 0
        self.signals = False
        self.seq = 0


class Prog:
    def __init__(self):
        self.ops = {e: [] for e in ENGS}
        self.dmabufs = []

    def op(self, eng, emit, reads=(), writes=(), dma=False, sembuf=None, extra=()):
        o = Op(eng, emit, dma, sembuf)
        o.seq = len(self.ops[eng])
        deps = {}

        def add(d):
            if not d.is_dma and d.eng == eng and eng in (PE, SP):
                return
            if dma and d.is_dma and d.sembuf is sembuf:
                return
            deps[id(d)] = d

        for b in reads:
            for w in b.writers:
                add(w)
            if b.excl:
                for r in b.readers:
                    if r.eng != eng:
                        add(r)
        for b in writes:
            for w in b.writers:
                add(w)
            for r in b.readers:
                if (not r.is_dma) and r.eng == eng:
                    continue
                add(r)
        for d in extra:
            add(d)
        best = {}
        dl = []
        for d in deps.values():
            if d.is_dma:
                dl.append(d)
            else:
                b_ = best.get(d.eng)
                if b_ is None or d.seq > b_.seq:
                    best[d.eng] = d
        o.deps = dl + list(best.values())
        for d in o.deps:
            d.signals = True
        for b in reads:
            b.readers.append(o)
        for b in writes:
            if dma and b.writers and b.writers[-1].is_dma and b.writers[-1].sembuf is sembuf and not b.readers:
                b.writers.append(o)
            else:
                b.writers = [o]
            b.readers = []
        if dma:
            if sembuf.ndma == 0:
                self.dmabufs.append(sembuf)
            sembuf.ndma += 1
            o.dma_cnt = sembuf.ndma
        self.ops[eng].append(o)
        return o

    def barrier(self):
        last = []
        for e in ENGS:
            for o in reversed(self.ops[e]):
                if (not o.is_dma) and o.emit is not None:
                    last.append(o)
                    break
        lastd = {}
        for e in ENGS:
            for o in self.ops[e]:
                if o.is_dma:
                    lastd[id(o.sembuf)] = o
        ex = last + list(lastd.values())
        for e in ENGS:
            o = Op(e, None, False, None)
            o.seq = len(self.ops[e])
            o.deps = [d for d in ex if d.is_dma or d.eng != e]
            for d in o.deps:
                d.signals = True
            self.ops[e].append(o)

    def finalize(self):
        for e in ENGS:
            k = 0
            for o in self.ops[e]:
                if (not o.is_dma) and o.signals:
                    k += 1
                    o.sig_idx = k

    def n_epochs(self, e):
        k = 0
        for o in self.ops[e]:
            if (not o.is_dma) and o.signals:
                k = o.sig_idx
        return (k + EPOCH - 1) // EPOCH + 1

    def emit_engine(self, e, eng, sems, dma_sems):
        waited = {}
        for o in self.ops[e]:
            need = {}
            for d in o.deps:
                if d.is_dma:
                    key = ("d", id(d.sembuf))
                    val = d.dma_cnt * 16
                    s = dma_sems[id(d.sembuf)]
                else:
                    ep = (d.sig_idx - 1) // EPOCH
                    key = ("e", d.eng, ep)
                    val = (d.sig_idx - 1) % EPOCH + 1
                    s = sems[d.eng][ep]
                if val > need.get(key, (0, None))[0]:
                    need[key] = (val, s)
            for key, (val, s) in need.items():
                if waited.get(key, 0) >= val:
                    continue
                waited[key] = val
                eng.wait_ge(s, val)
            if o.emit is None:
                continue
            ins = o.emit(eng)
            if o.is_dma:
                ins.then_inc(dma_sems[id(o.sembuf)], 16)
            elif o.signals:
                ins.then_inc(sems[e][(o.sig_idx - 1) // EPOCH], 1)


def _t5_bucket(rel):
    rel = np.asarray(rel, dtype=np.int64)
    nb = 16
    max_exact = 8
    ret = (rel > 0).astype(np.int64) * nb
    n = np.abs(rel)
    nf = np.maximum(n, 1).astype(np.float32)
    large = max_exact + (np.log(nf / np.float32(max_exact)) / np.float32(math.log(128 / max_exact))
                         * np.float32(nb - max_exact)).astype(np.int64)
    large = np.minimum(large, nb - 1)
    return ret + np.where(n < max_exact, n, large)


def _rope_tables(pos):
    p = np.maximum(pos, 0)
    row = (p // 64).astype(np.float32)
    col = (p % 64).astype(np.float32)
    freqs = (np.float32(10000.0) ** (-np.arange(0, 64, 2, dtype=np.float32) / np.float32(64))).astype(np.float32)
    ar = row[:, None] * freqs[None, :]
    ac = col[:, None] * freqs[None, :]
    cos = np.concatenate([np.cos(ar), np.cos(ar), np.cos(ac), np.cos(ac)], -1).astype(np.float32)
    sin = np.concatenate([np.sin(ar), np.sin(ar), np.sin(ac), np.sin(ac)], -1).astype(np.float32)
    return cos, sin


def _near_d(i, s, kind):
    if kind == "pre":
        return -128 if i == 0 else None
    if kind == "post":
        return 512 if i == 3 else None
    d = 128 * s - 512 * i
    return d if -128 <= d <= 512 else None


def _sel_table(slot_k0, q0s):
    ns = len(slot_k0)
    sel = np.zeros((33, 4, ns), np.float32)
    selb = np.zeros((33, 4, ns), np.float32)
    return sel, selb


def _prompt_slots(j):
    Q0 = 2048 * j
    block = [Q0 + 128 * s for s in range(16)]
    pre = Q0 - 128 if Q0 - 128 >= 0 else None
    post = Q0 + 2048 if Q0 + 2048 < 8192 else None
    neg = [k0 for k0 in range(0, 8192, 128) if k0 < Q0 - 128]
    pos = [k0 for k0 in range(0, 8192, 128) if k0 > Q0 + 2048]
    if len(neg) % 2 and pos:
        neg = neg + [None]
    rest = neg + pos
    assert len(rest) <= 47
    rest = rest + [None] * (47 - len(rest))
    slots = block + [pre] + [post] + rest
    slots = slots + [None] * (NSLOT_PP - len(slots))
    assert len([s for s in slots[:NSLOT_P] if s is not None]) == 64
    return slots


def _fb_sel(slots, Q0, nslot):
    sel = np.zeros((33, 4, nslot), np.float32)
    for i in range(4):
        q0 = Q0 + 512 * i
        for s in range(nslot):
            k0 = slots[s]
            if k0 is None:
                continue
            d = k0 - q0
            if -128 <= d <= 512:
                continue
            rel = (k0 + np.arange(128))[:, None] - (q0 + np.arange(512))[None, :]
            b = _t5_bucket(rel)
            assert b.min() == b.max()
            sel[int(b.min()), i, s] = 1.0
        for s in range(18, nslot):
            if slots[s] is None:
                p_ = 18 + ((s - 18) ^ 1)
                if p_ < nslot and slots[p_] is not None:
                    sel[:, i, s] = sel[:, i, p_]
    return sel


class Job:
    pass


def build_program(n_sample=4, with_prompt=True, phases="0ABC"):
    nc = bass.Bass("TRN2", target_bir_lowering=False)
    P = Prog()
    jobs = []
    for s in range(n_sample):
        jb = Job()
        jb.name = "s%d" % s
        jb.nslot = 16
        jb.ntile = 4
        jb.fb0 = 0
        jb.rope0 = 0
        jb.prompt = False
        jobs.append(jb)
    if with_prompt:
        jb = Job()
        jb.name = "p"
        jb.nslot = NSLOT_P
        jb.ntile = 17
        jb.fb0 = 64
        jb.rope0 = 4
        jb.prompt = True
        jobs.append(jb)
    xt0 = 0
    qt0 = 0
    for jb in jobs:
        jb.xt0 = xt0
        xt0 += jb.ntile
        jb.qt0 = qt0
        qt0 += 4
    NXT = xt0
    NQT = qt0
    NROPE = 4 + (17 if with_prompt else 0)
    NFB = 64 + (4 * NSLOT_P if with_prompt else 0)

    def din(name, shape, dt=F32):
        return nc.dram_tensor(name, list(shape), dt, kind="ExternalInput").ap()

    def dscr(name, shape, dt=BF16):
        return nc.dram_tensor(name, list(shape), dt, kind="Internal").ap()

    xT = din("xT", [NXT, 128, KC, NT])
    xtok = din("xtok", [NQT, 128, 4, D])
    ropeT = din("ropeT", [NROPE, 128, 2, NT])
    vflag = din("vflag", [NROPE, 128, 4])
    w_in = din("w_in", [D, 6656])
    w_pa = din("w_pa", [D, D])
    w_pb = din("w_pb", [D, D])
    w_o = din("w_o", [D, D])
    w_g = din("w_g", [D, DFF])
    w_u = din("w_u", [D, DFF])
    w_d = din("w_d", [DFF, D])
    lam4 = din("lam4", [4, 64])
    subln_g = din("subln_g", [1, 128])
    qn_g = din("qn_g", [1, 128])
    kn_g = din("kn_g", [1, 128])
    rel_bias = din("rel_bias", [32, 8])
    lnp = din("lnp", [4, D])
    oh4 = din("oh4", [32, 1280])
    selT = din("selT", [33, NFB])
    cmat = din("cmat", [128, 4, 128])
    y = nc.dram_tensor("y", [NQT, 128, 4, D], F32, kind="ExternalOutput").ap()

    wb_in = dscr("wb_in", [13, 128, KC, NT])
    wb_pa = dscr("wb_pa", [2, 128, KC, NT])
    wb_pb = dscr("wb_pb", [2, 128, KC, NT])
    wb_o = dscr("wb_o", [2, 128, KC, NT])
    wb_g = dscr("wb_g", [6, 128, KC, NT])
    wb_u = dscr("wb_u", [6, 128, KC, NT])
    wb_d = dscr("wb_d", [2, 128, NFC, NT])
    g4d = dscr("g4d", [8, 1280], F32)
    for jb in jobs:
        S = jb.ntile * NT
        ns = jb.ntile * 4
        jb.QA = dscr("QA" + jb.name, [8, 2, 128, 2048])
        jb.QB = dscr("QB" + jb.name, [8, 128, 2048])
        jb.KA = dscr("KA" + jb.name, [8, 128, S])
        jb.KB = dscr("KB" + jb.name, [2, 128, S])
        jb.VA = dscr("VA" + jb.name, [8, 128, ns, VW])
        jb.VB = dscr("VB" + jb.name, [2, 128, ns, VW])
        jb.OAT = dscr("OAT" + jb.name, [8, 128, 2048])
        jb.OBT = dscr("OBT" + jb.name, [8, 128, 2048])
        jb.S = S
        jb.ns = ns
        jb.kvstores = []
        jb.ostores = []

    es = ExitStack()
    arena = es.enter_context(nc.sbuf_tensor("arena", [128, 48128], F32))
    ps = es.enter_context(nc.psum_tensor("ps", [128, 8, 512], F32))
    psb = [Buf("psb%d" % i, True) for i in range(8)]

    class Alloc:
        def __init__(self, base):
            self.off = base

        def __call__(self, shape, dt, name=None, np_=None):
            n = 1
            for s_ in shape[1:]:
                n *= s_
            words = (n * (4 if dt == F32 else 2) + 3) // 4
            words = (words + 7) // 8 * 8
            ap = arena[0:shape[0], self.off:self.off + words]
            self.off += words
            assert self.off <= 48128, ("sbuf overflow", name, self.off)
            if dt == BF16:
                ap = ap.bitcast(BF16)
            ap = ap[:, 0:n]
            if len(shape) == 3:
                ap = ap.rearrange("p (a b) -> p a b", a=shape[1])
            elif len(shape) == 4:
                ap = ap.rearrange("p (a b c) -> p a b c", a=shape[1], b=shape[2])
            return ap

    def mm(out, lhsT, rhs, start, stop, reads, writes):
        return P.op(PE, lambda e: e.matmul(out, lhsT=lhsT, rhs=rhs, start=start, stop=stop), reads, writes)

    def tr(out, in_, ident, reads, writes):
        return P.op(PE, lambda e: e.transpose(out, in_, ident), reads, writes)

    def act(out, in_, func, reads, writes, bias=None, scale=1.0, accum=None, eng=ACT):
        def f(e):
            kw = {}
            if bias is not None:
                kw["bias"] = bias
            if accum is not None:
                kw["accum_out"] = accum
            return e.activation(out=out, in_=in_, func=func, scale=scale, **kw)
        return P.op(eng, f, reads, writes)

    def tt(eng, out, in0, in1, op, reads, writes):
        return P.op(eng, lambda e: e.tensor_tensor(out=out, in0=in0, in1=in1, op=op), reads, writes)

    def ts(eng, out, in0, s1, s2, op0, op1, reads, writes):
        if op1 is None:
            return P.op(eng, lambda e: e.tensor_scalar(out=out, in0=in0, scalar1=s1, scalar2=None, op0=op0), reads, writes)
        return P.op(eng, lambda e: e.tensor_scalar(out=out, in0=in0, scalar1=s1, scalar2=s2, op0=op0, op1=op1), reads, writes)

    def stt(eng, out, in0, scalar, in1, op0, op1, reads, writes):
        return P.op(eng, lambda e: e.scalar_tensor_tensor(out=out, in0=in0, scalar=scalar, in1=in1, op0=op0, op1=op1), reads, writes)

    def cp(eng, out, in_, reads, writes):
        if eng == ACT:
            return P.op(eng, lambda e: e.copy(out=out, in_=in_), reads, writes)
        return P.op(eng, lambda e: e.tensor_copy(out=out, in_=in_), reads, writes)

    def dma(eng, out, in_, reads, writes, sembuf, extra=()):
        return P.op(eng, lambda e: e.dma_start(out=out, in_=in_), reads, writes, dma=True, sembuf=sembuf, extra=extra)

    class Ring:
        def __init__(self, n, mk):
            self.items = [mk(i) for i in range(n)]
            self.i = 0

        def next(self):
            it = self.items[self.i % len(self.items)]
            self.i += 1
            return it

    class Stream:
        def __init__(self, ring, specs, issue, look):
            self.ring, self.specs, self.issue, self.look = ring, specs, issue, look
            self.nissued = 0
            self.k = 0
            self.items = []

        def _upto(self, n):
            while self.nissued < min(n, len(self.specs)):
                v, b = self.ring.next()
                self.issue(v, b, self.specs[self.nissued])
                self.items.append((v, b))
                self.nissued += 1

        def prefetch(self, n=1):
            self._upto(self.k + n)

        def get(self):
            self._upto(self.k + 1 + self.look)
            it = self.items[self.k]
            self.k += 1
            return it

    A0 = Alloc(0)
    cm = A0([128, 4, 128], BF16)
    B_cm = Buf("cm")
    ident, antiI, rotM, onesb = cm[:, 0, :], cm[:, 1, :], cm[:, 2, :], cm[:, 3, :]
    ones32 = A0([128, 128], F32)
    B_ones32 = Buf("ones32")
    gcols = A0([128, 4], F32)
    B_gcols = Buf("gcols")
    lamc = A0([128, 4], F32)
    B_lamc = Buf("lamc")
    gsub = A0([128, 128], F32)
    B_gsub = Buf("gsub")
    epsc = A0([128, 2], F32)
    B_epsc = Buf("epsc")
    SMALL_END = A0.off

    A1 = Alloc(SMALL_END)
    strip = A1([128, 8, 1152], BF16)
    B_strip = Buf("strip")
    FB = A1([128, NFB, 9], F32)
    B_FB = Buf("FB")
    MID_END = A1.off

    AT = Alloc(MID_END + 30600)
    B_w = {n_: Buf("wscr_" + n_) for n_ in ("in", "pa", "pb", "o", "g", "u", "d")}
    for (n_, src, dst, ncol) in (("in", w_in, wb_in, 6656), ("g", w_g, wb_g, DFF), ("u", w_u, wb_u, DFF),
                                 ("pa", w_pa, wb_pa, D), ("pb", w_pb, wb_pb, D), ("o", w_o, wb_o, D)):
        ng = ncol // NT
        for kc in range(KC):
            dma(POOL, dst[0:ng, :, kc, :].rearrange("g p c -> p g c"),
                src[kc * 128:(kc + 1) * 128, 0:ng * NT].rearrange("p (g c) -> p g c", c=NT), [], [B_w[n_]], B_w[n_])
            if ncol % NT:
                dma(POOL, dst[ng, :, kc, 0:ncol % NT], src[kc * 128:(kc + 1) * 128, ng * NT:ncol], [], [B_w[n_]], B_w[n_])
    for fc in range(NFC):
        dma(POOL, wb_d[:, :, fc, :].rearrange("h p c -> p h c"),
            w_d[fc * 128:(fc + 1) * 128, :].rearrange("p (h c) -> p h c", c=NT), [], [B_w["d"]], B_w["d"])

    cm32 = AT([128, 4, 128], F32)
    B_cm32 = Buf("cm32")
    dma(SP, cm32, cmat, [], [B_cm32], B_cm32)
    cp(DVE, cm, cm32, [B_cm32], [B_cm])
    P.op(DVE, lambda e: e.memset(ones32, 1.0), [], [B_ones32])
    P.op(DVE, lambda e: e.memset(epsc[:, 0:1], RMS_EPS), [], [B_epsc])
    P.op(DVE, lambda e: e.memset(epsc[:, 1:2], LN_EPS), [], [B_epsc])
    g4c = AT([128, 4], F32)
    B_g4c = Buf("g4c")
    for ci, gsrc in ((0, qn_g), (2, kn_g)):
        dma(SP, g4c[:, ci:ci + 1], gsrc.rearrange("o d -> d o"), [], [B_g4c], B_g4c)
        for seg in range(4):
            srcseg = seg ^ 1
            dma(SP, g4c[seg * 32:(seg + 1) * 32, ci + 1:ci + 2],
                gsrc[:, srcseg * 32:(srcseg + 1) * 32].rearrange("o d -> d o"), [], [B_g4c], B_g4c)
    cp(DVE, gcols, g4c, [B_g4c], [B_gcols])
    lam_sb = AT([128, 4, 64], F32)
    B_lam = Buf("lam_sb")
    dma(SP, lam_sb, lam4.partition_broadcast(128), [], [B_lam], B_lam)
    lprod = AT([128, 2, 64], F32)
    B_lprod = Buf("lprod")
    lsum = AT([128, 2], F32)
    B_lsum = Buf("lsum")
    lexp = AT([128, 2], F32)
    B_lexp = Buf("lexp")
    tt(DVE, lprod[:, 0, :], lam_sb[:, 0, :], lam_sb[:, 1, :], ALU.mult, [B_lam], [B_lprod])
    tt(DVE, lprod[:, 1, :], lam_sb[:, 2, :], lam_sb[:, 3, :], ALU.mult, [B_lam], [B_lprod])
    P.op(DVE, lambda e: e.reduce_sum(out=lsum, in_=lprod, axis=AX.X), [B_lprod], [B_lsum])
    act(lexp, lsum, AF.Exp, [B_lsum], [B_lexp])
    stt(DVE, lamc[:, 0:1], lexp[:, 0:1], float(LAM_INIT), lexp[:, 1:2], ALU.add, ALU.subtract, [B_lexp], [B_lamc])
    ts(DVE, lamc[:, 1:2], lamc[:, 0:1], -1.0, None, ALU.mult, None, [B_lamc], [B_lamc])
    gsub_raw = AT([128, 128], F32)
    B_gsr = Buf("gsub_raw")
    dma(SP, gsub_raw, subln_g[0].partition_broadcast(128), [], [B_gsr], B_gsr)
    ts(DVE, gsub, gsub_raw, float(1.0 - LAM_INIT), None, ALU.mult, None, [B_gsr], [B_gsub])
    rbx = AT([33, 9], F32)
    B_rbx = Buf("rbx")
    P.op(DVE, lambda e: e.memset(rbx, 0.0), [], [B_rbx])
    P.op(DVE, lambda e: e.memset(rbx[32:33, :], GHOST), [], [B_rbx])
    dma(SP, rbx[0:32, 0:8], rel_bias, [], [B_rbx], B_rbx)
    oh4_sb = AT([32, 1280], F32)
    B_oh4 = Buf("oh4")
    dma(SP, oh4_sb, oh4, [], [B_oh4], B_oh4)
    g4_sb = AT([8, 1280], F32)
    B_g4 = Buf("g4")
    for c0, cw in ((0, 512), (512, 512), (1024, 256)):
        bk = (c0 // 512)
        mm(ps[0:8, bk, 0:cw], rbx[0:32, 0:8], oh4_sb[:, c0:c0 + cw], True, True, [B_rbx, B_oh4], [psb[bk]])
        ts(DVE, g4_sb[:, c0:c0 + cw], ps[0:8, bk, 0:cw], 8.0, None, ALU.mult, None, [psb[bk]], [B_g4])
    B_g4d = Buf("g4d")
    st_g4 = dma(SP, g4d, g4_sb, [B_g4], [B_g4d], B_g4d)
    hank = bass.AP(tensor=g4d.tensor, offset=0, ap=[[1, 128], [1280, 8], [1, 1152]])
    dma(POOL, strip, hank, [B_g4d], [B_strip], B_strip)
    sel_sb = AT([33, NFB], F32)
    B_sel = Buf("sel")
    dma(SP, sel_sb, selT, [], [B_sel], B_sel)
    CH = 56
    selx = AT([33, CH, 9], F32)
    B_selx = Buf("selx")
    for c0 in range(0, NFB, CH):
        cw = min(CH, NFB - c0)
        bk = 4 + (c0 // CH) % 4
        tt(DVE, selx[:, 0:cw, :], sel_sb[:, c0:c0 + cw].unsqueeze(2).to_broadcast([33, cw, 9]),
           rbx[:, :].unsqueeze(1).to_broadcast([33, cw, 9]), ALU.mult, [B_sel, B_rbx], [B_selx])
        mm(ps[:, bk, 0:cw * 9], ones32[0:33, :], selx[:, 0:cw, :].rearrange("p a b -> p (a b)"), True, True,
           [B_ones32, B_selx], [psb[bk]])
        cp(DVE, FB[:, c0:c0 + cw, :].rearrange("p a b -> p (a b)"), ps[:, bk, 0:cw * 9], [psb[bk]], [B_FB])

    AA = Alloc(MID_END)
    xb_ring = Ring(2, lambda i: (AA([128, KC, NT], BF16), Buf("xb%d" % i)))
    w_ring = Ring(4, lambda i: (AA([128, KC, NT], BF16), Buf("w%d" % i)))
    rope_ring = Ring(2, lambda i: (AA([128, 2, NT], F32), Buf("rope%d" % i)))
    vf_ring = Ring(2, lambda i: (AA([128, 4], F32), Buf("vf%d" % i)))
    fst_ring = Ring(2, lambda i: (AA([128, 4, NT], BF16), Buf("fst%d" % i)))
    qst_ring = Ring(2, lambda i: (AA([128, 4, 2, NT], BF16), Buf("qst%d" % i)))
    vst_ring = Ring(2, lambda i: (AA([128, 8, 4, VW], BF16), Buf("vst%d" % i)))
    vbst_ring = Ring(2, lambda i: (AA([128, 2, 4, VW], BF16), Buf("vbst%d" % i)))
    sq_ring = Ring(2, lambda i: (AA([128, NT], BF16), Buf("sq%d" % i)))
    qbf_ring = Ring(2, lambda i: (AA([128, NT], BF16), Buf("qbf%d" % i)))
    rstd_ring = Ring(2, lambda i: (AA([128, NT], F32), Buf("rstd%d" % i)))
    t1_ring = Ring(2, lambda i: (AA([128, NT], F32), Buf("t1_%d" % i)))
    t2_ring = Ring(2, lambda i: (AA([128, NT], F32), Buf("t2_%d" % i)))
    assert AA.off <= MID_END + 30600, AA.off
    mainps = Ring(4, lambda i: i)
    ssps = Ring(2, lambda i: 4 + i)
    rotps = Ring(2, lambda i: 6 + i)
    evac_flip = [0]

    def evac_eng():
        evac_flip[0] ^= 1
        return ACT if evac_flip[0] else DVE

    def a_groups(t):
        groups = []
        if t < 4:
            groups += [("QA", 0), ("QA", 1)]
        groups += [("KA", 0), ("KA", 1), ("VA", 0), ("VA", 1)]
        if t < 4:
            groups += [("QB", 0), ("QB", 1)]
        groups += [("KVB", 0)]
        return groups

    A_COL = {"QA": OFF_QA, "KA": OFF_KA, "VA": OFF_VA, "QB": OFF_QB, "KVB": OFF_KB}
    if "A" in phases:
        wspecsA = []
        xspecsA = []
        for jb in jobs:
            for t in range(jb.ntile):
                xspecsA.append(jb.xt0 + t)
                for (kind, gi) in a_groups(t):
                    c0_ = A_COL[kind] + gi * 512
                    wspecsA.append((wb_in[c0_ // NT], NT, KC, B_w["in"]))
        wsA = Stream(w_ring, wspecsA, lambda v, b, sp: dma(SP, v[:, 0:sp[2], 0:sp[1]], sp[0], [sp[3]], [b], b), 2)
        xsA = Stream(xb_ring, xspecsA, lambda v, b, sp: dma(POOL, v, xT[sp], [], [b], b), 1)
        for (v_, b_) in vst_ring.items + vbst_ring.items:
            P.op(POOL, lambda e, v_=v_: e.memset(v_[:, :, :, 128:VW], 0.0), [], [b_])
        for (v_, b_) in qst_ring.items:
            P.op(POOL, lambda e, v_=v_: e.memset(v_, 0.0), [], [b_])
        for jb in jobs:
            for t in range(jb.ntile):
                is_q = t < 4
                xv, xbuf = xsA.get()
                rv, rbuf = rope_ring.next()
                dma(SP, rv, ropeT[jb.rope0 + t], [], [rbuf], rbuf)
                tok0 = t * NT
                groups = a_groups(t)
                vst, vbuf = vst_ring.next()
                vfv, vfbuf = vf_ring.next()
                dma(SP, vfv, vflag[jb.rope0 + t], [], [vfbuf], vfbuf)
                P.op(POOL, lambda e, vst=vst, vfv=vfv: e.tensor_copy(out=vst[:, :, :, 128:129],
                                                                    in_=vfv.unsqueeze(1).unsqueeze(3).to_broadcast([128, 8, 4, 1])),
                     [vfbuf], [vbuf])
                for (kind, gi) in groups:
                    col0 = {"QA": OFF_QA, "KA": OFF_KA, "VA": OFF_VA, "QB": OFF_QB, "KVB": OFF_KB}[kind] + gi * 512
                    wv, wbuf = wsA.get()
                    if kind == "QA":
                        qst, qsbuf = qst_ring.next()
                        for hh in range(4):
                            bk = mainps.next()
                            for kc in range(KC):
                                mm(ps[:, bk, :], wv[:, kc, hh * 128:(hh + 1) * 128], xv[:, kc, :], kc == 0, kc == KC - 1,
                                   [wbuf, xbuf], [psb[bk]])
                            cp(ACT, qst[0:64, hh, 0, :], ps[0:64, bk, :], [psb[bk]], [qsbuf])
                            cp(DVE, qst[64:128, hh, 1, :], ps[64:128, bk, :], [psb[bk]], [qsbuf])
                        dst = jb.QA[gi * 4:(gi + 1) * 4, :, :, tok0:tok0 + NT].rearrange("h c d t -> d h c t")
                        o = dma(SP, dst, qst, [qsbuf], [], qsbuf)
                        jb.kvstores.append(o)
                    elif kind == "KA":
                        fst, fbuf = fst_ring.next()
                        for hh in range(4):
                            bk = mainps.next()
                            for kc in range(KC):
                                mm(ps[:, bk, :], wv[:, kc, hh * 128:(hh + 1) * 128], xv[:, kc, :], kc == 0, kc == KC - 1,
                                   [wbuf, xbuf], [psb[bk]])
                            cp(evac_eng(), fst[:, hh, :], ps[:, bk, :], [psb[bk]], [fbuf])
                        dst = jb.KA[gi * 4:(gi + 1) * 4, :, tok0:tok0 + NT].rearrange("h d t -> d h t")
                        o = dma(SP, dst, fst, [fbuf], [], fbuf)
                        jb.kvstores.append(o)
                    elif kind == "VA":
                        for sub in range(4):
                            bk = mainps.next()
                            for kc in range(KC):
                                mm(ps[:, bk, :], xv[:, kc, sub * 128:(sub + 1) * 128], wv[:, kc, :], kc == 0, kc == KC - 1,
                                   [wbuf, xbuf], [psb[bk]])
                            cp(evac_eng(), vst[:, gi * 4:(gi + 1) * 4, sub, 0:128],
                               ps[:, bk, :].rearrange("p (h e) -> p h e", h=4), [psb[bk]], [vbuf])
                        if gi == 1:
                            dst = jb.VA[:, :, t * 4:(t + 1) * 4, :].rearrange("h k s e -> k h (s e)")
                            o = dma(SP, dst, vst.rearrange("p h s e -> p h (s e)"), [vbuf], [], vbuf)
                            jb.kvstores.append(o)
                    else:
                        nh = 4 if kind == "QB" else 2
                        gi0 = 0 if kind == "QB" else 2
                        fst, fbuf = fst_ring.next()
                        for hh in range(nh):
                            bk = mainps.next()
                            for kc in range(KC):
                                mm(ps[:, bk, :], wv[:, kc, hh * 128:(hh + 1) * 128], xv[:, kc, :], kc == 0, kc == KC - 1,
                                   [wbuf, xbuf], [psb[bk]])
                            sq, sqb = sq_ring.next()
                            qbf, qbb = qbf_ring.next()
                            act(sq, ps[:, bk, :], AF.Square, [psb[bk]], [sqb])
                            cp(ACT, qbf, ps[:, bk, :], [psb[bk]], [qbb])
                            sbk = ssps.next()
                            rbk = rotps.next()
                            mm(ps[:, sbk, :], onesb, sq, True, True, [B_cm, sqb], [psb[sbk]])
                            mm(ps[:, rbk, :], rotM, qbf, True, True, [B_cm, qbb], [psb[rbk]])
                            rstd, rsb = rstd_ring.next()
                            act(rstd, ps[:, sbk, :], AF.Ln, [psb[sbk], B_epsc], [rsb], bias=epsc[:, 0:1], scale=float(1.0 / 128.0))
                            act(rstd, rstd, AF.Exp, [rsb], [rsb], scale=-0.5)
                            t1, t1b = t1_ring.next()
                            t2, t2b = t2_ring.next()
                            stt(DVE, t1, ps[:, bk, :], gcols[:, gi0:gi0 + 1], rv[:, 0, :], ALU.mult, ALU.mult,
                                [psb[bk], B_gcols, rbuf], [t1b])
                            stt(POOL if False else DVE, t2, ps[:, rbk, :], gcols[:, gi0 + 1:gi0 + 2], rv[:, 1, :], ALU.mult, ALU.mult,
                                [psb[rbk], B_gcols, rbuf], [t2b])
                            ELE = DVE if os.environ.get('NOPOOL') else POOL
                            tt(ELE, t1, t1, t2, ALU.add, [t1b, t2b], [t1b])
                            tt(ELE, fst[:, hh, :], t1, rstd, ALU.mult, [t1b, rsb], [fbuf])
                        if kind == "QB":
                            dst = jb.QB[gi * 4:(gi + 1) * 4, :, tok0:tok0 + NT].rearrange("h d t -> d h t")
                            o = dma(SP, dst, fst, [fbuf], [], fbuf)
                        else:
                            dst = jb.KB[:, :, tok0:tok0 + NT].rearrange("h d t -> d h t")
                            o = dma(SP, dst, fst[:, 0:2, :], [fbuf], [], fbuf)
                        jb.kvstores.append(o)
                        if kind == "KVB":
                            vbst, vbbuf = vbst_ring.next()
                            P.op(POOL, lambda e, vbst=vbst, vfv=vfv: e.tensor_copy(out=vbst[:, :, :, 128:129],
                                                                                  in_=vfv.unsqueeze(1).unsqueeze(3).to_broadcast([128, 2, 4, 1])),
                                 [vfbuf], [vbbuf])
                            for sub in range(4):
                                bk = mainps.next()
                                for kc in range(KC):
                                    mm(ps[:, bk, 0:256], xv[:, kc, sub * 128:(sub + 1) * 128], wv[:, kc, 256:512], kc == 0, kc == KC - 1,
                                       [wbuf, xbuf], [psb[bk]])
                                cp(evac_eng(), vbst[:, :, sub, 0:128],
                                   ps[:, bk, 0:256].rearrange("p (h e) -> p h e", h=2), [psb[bk]], [vbbuf])
                            dst = jb.VB[:, :, t * 4:(t + 1) * 4, :].rearrange("h k s e -> k h (s e)")
                            o = dma(SP, dst, vbst.rearrange("p h s e -> p h (s e)"), [vbbuf], [], vbbuf)
                            jb.kvstores.append(o)
        P.barrier()

    if "B" in phases:
        AB = Alloc(MID_END)
        SMAX = max(jb.S for jb in jobs)
        NSMAX = max(jb.ns for jb in jobs)
        k_ring = Ring(2, lambda i: (AB([128, SMAX], BF16), Buf("K%d" % i)))
        v_ring = Ring(2, lambda i: (AB([128, NSMAX, VW], BF16), Buf("V%d" % i)))
        q_ring = Ring(2, lambda i: (AB([128, 2, 2048], BF16), Buf("Q%d" % i)))
        pt_ring = Ring(3, lambda i: (AB([128, 2 * NT], BF16), Buf("PT%d" % i)))
        rz_ring = Ring(2, lambda i: (AB([128, 4, 1], F32), Buf("rz%d" % i)))
        o_ring = Ring(4, lambda i: (AB([128, 4, 128], F32), Buf("o%d" % i)))
        osq = AB([128, 4, 128], F32)
        B_osq = Buf("osq")
        ssum_ring = Ring(2, lambda i: (AB([128, 12], F32), Buf("ssum%d" % i)))
        onb_ring = Ring(3, lambda i: (AB([128, 4, 128], BF16), Buf("onb%d" % i)))
        ot_ring = Ring(2, lambda i: (AB([128, NT], BF16), Buf("ot%d" % i)))
        s_ring = Ring(3, lambda i: 2 * i) if PAIR else Ring(3, lambda i: i)
        LOOK = 2
        ST_EARLY = (1, 2, 3) if PAIR else (1, 4, 6)
        ST_FIN = 5 if PAIR else 9
        acc_ring = Ring(1, lambda i: 6) if PAIR else Ring(2, lambda i: 4 + 2 * i)
        accs_ring = Ring(3, lambda i: (AB([128, 4, 129], F32), Buf("accs%d" % i)))

        def acc_out(ab, accv):
            if not PAIR:
                return accv, [psb[ab], psb[ab + 1]]
            a_, ab_ = accs_ring.next()
            cp(DVE, a_, accv[:, :, 0:129], [psb[ab], psb[ab + 1]], [ab_])
            return a_, [ab_]

        stages = {}

        def defer(k, f):
            stages.setdefault(k, []).append(f)

        def run_stage(k, *a):
            for f in stages.pop(k, []):
                f(*a)

        def slot_class(jb, i, s):
            if s < 16:
                d = 128 * s - 512 * i
                if -128 <= d <= 512:
                    return ("near", d)
                return ("neg", None) if d < -128 else ("pos", None)
            if s == 16:
                return ("near", -128) if i == 0 else ("dyn16", None)
            if s == 17:
                return ("near", 512) if i == 3 else ("dyn17", None)
            return ("rest", None)

        def attention(jb, Kv, Kb, Vv, Vb, Qv, Qb, i, krows, h_fb, strip_h):
            ab = acc_ring.next()
            accv = ps[:, ab:ab + 2, :].rearrange("p a (s c) -> p (a s) c", c=256)
            nsl = jb.nslot
            groups = []
            s = 0
            while s < nsl:
                pair = False
                if s + 1 < nsl:
                    if strip_h is None:
                        pair = True
                    else:
                        c0_, c1_ = slot_class(jb, i, s)[0], slot_class(jb, i, s + 1)[0]
                        if s < 16 and s + 1 < 16:
                            pair = (c0_ == c1_)
                        elif s >= 18 and (s - 18) % 2 == 0:
                            pair = True
                if pair and PAIR:
                    groups.append((s, s + 1))
                    s += 2
                else:
                    groups.append((s,))
                    s += 1

            def qk(g):
                b0 = s_ring.next()
                for gi_, s in enumerate(g):
                    bk = b0 + gi_
                    d = slot_class(jb, i, s)[1] if strip_h is not None else None
                    near = d is not None
                    mm(ps[:, bk, :], Kv[krows[0]:krows[1], s * 128:(s + 1) * 128], Qv[krows[0]:krows[1], i * NT:(i + 1) * NT],
                       True, not near, [Kb, Qb], [psb[bk]])
                    if near:
                        off = 512 - d
                        mm(ps[:, bk, :], antiI, strip_h[:, off:off + NT], False, True, [B_cm, B_strip], [psb[bk]])
                return b0

            pend = [qk(groups[g_]) for g_ in range(min(LOOK, len(groups)))]
            for gidx, g in enumerate(groups):
                if gidx + LOOK < len(groups):
                    pend.append(qk(groups[gidx + LOOK]))
                for k_ in ST_EARLY:
                    if gidx == k_:
                        run_stage(k_)
                b0 = pend.pop(0)
                n = len(g)
                pt, ptb = pt_ring.next()
                rd = [psb[b0 + k_] for k_ in range(n)]
                src = ps[:, b0:b0 + n, :].rearrange("p a c -> p (a c)")
                if strip_h is not None:
                    fbidx = jb.fb0 + i * nsl + g[0]
                    act(pt[:, 0:n * NT], src, AF.Exp, rd + [B_FB], [ptb], bias=FB[:, fbidx, h_fb:h_fb + 1],
                        scale=float(1.0 / krows[2]))
                else:
                    act(pt[:, 0:n * NT], src, AF.Exp, rd, [ptb], scale=float(1.0 / krows[2]))
                if gidx == ST_FIN:
                    run_stage(ST_FIN, b0 if PAIR else 3)
                for gi_, s in enumerate(g):
                    for sub in range(4):
                        P.op(PE, lambda e, sub=sub, s=s, pt=pt, gi_=gi_: e.matmul(
                            accv[:, sub, 0:129], lhsT=pt[:, gi_ * NT + sub * 128:gi_ * NT + (sub + 1) * 128],
                            rhs=Vv[:, s, 0:129], start=(s == 0 and sub % 2 == 0),
                            stop=(s == nsl - 1), skip_group_check=True),
                             [ptb, Vb], [psb[ab], psb[ab + 1]])
            return ab, accv

        def finish(jb, h_chunk, dstT, i, onb, onbb, extra_stores, TB):
            ps_bf = ps[:, TB, :].bitcast(BF16)
            for sub in range(4):
                tr(ps_bf[:, sub * 128:(sub + 1) * 128], onb[:, sub, :], ident, [onbb, B_cm], [psb[TB]])
            ot, otb = ot_ring.next()
            cp(DVE, ot, ps_bf[:, 0:512], [psb[TB]], [otb])
            o = dma(SP, dstT[h_chunk, :, i * NT:(i + 1) * NT], ot, [otb], [], otb)
            extra_stores.append(o)

        units = []
        for jb in jobs:
            for h in range(8):
                units.append((jb, "A", h))
            for g in range(2):
                for hq in range(4):
                    units.append((jb, "B", g * 4 + hq))
        loaded = {}
        kvcur = {}

        def issue_loads(u):
            jb, kind, h = u
            if kind == "A":
                Kv, Kb = k_ring.next()
                Vv, Vb = v_ring.next()
                Qv, Qb = q_ring.next()
                dma(SP, Kv[:, 0:jb.S], jb.KA[h], [], [Kb], Kb, extra=jb.kvstores)
                dma(SP, Vv[:, 0:jb.ns, :], jb.VA[h], [], [Vb], Vb, extra=jb.kvstores)
                dma(SP, Qv, jb.QA[h].rearrange("c d t -> d c t"), [], [Qb], Qb, extra=jb.kvstores)
                loaded[id(u)] = (Kv, Kb, Vv, Vb, Qv, Qb)
            else:
                g = h // 4
                if h % 4 == 0:
                    Kv, Kb = k_ring.next()
                    Vv, Vb = v_ring.next()
                    dma(SP, Kv[:, 0:jb.S], jb.KB[g], [], [Kb], Kb, extra=jb.kvstores)
                    dma(SP, Vv[:, 0:jb.ns, :], jb.VB[g], [], [Vb], Vb, extra=jb.kvstores)
                    kvcur[(jb.name, g)] = (Kv, Kb, Vv, Vb)
                Kv, Kb, Vv, Vb = kvcur[(jb.name, g)]
                Qv, Qb = q_ring.next()
                dma(SP, Qv[:, 0, :], jb.QB[h], [], [Qb], Qb, extra=jb.kvstores)
                loaded[id(u)] = (Kv, Kb, Vv, Vb, Qv, Qb)

        issue_loads(units[0])
        for ui, u in enumerate(units):
            jb, kind, h = u
            if ui + 1 < len(units):
                issue_loads(units[ui + 1])
            Kv, Kb, Vv, Vb, Qv, Qb = loaded.pop(id(u))
            if kind == "A":
                for i in range(4):
                    res = []
                    for c in range(2):
                        ab, accv = attention(jb, Kv, Kb, Vv, Vb, Qv[:, c, :], Qb, i, (0, 128, 8.0), h, strip[:, h, :])
                        o_, ob_ = o_ring.next()
                        asrc, ards = acc_out(ab, accv)

                        def norm(asrc=asrc, ards=ards, o_=o_, ob_=ob_):
                            rz, rzb = rz_ring.next()
                            P.op(DVE, lambda e, rz=rz, asrc=asrc: e.reciprocal(out=rz, in_=asrc[:, :, 128:129]), ards, [rzb])
                            tt(DVE, o_, asrc[:, :, 0:128], rz.to_broadcast([128, 4, 128]), ALU.mult, ards + [rzb], [ob_])
                        defer(ST_EARLY[0], norm)
                        res.append((o_, ob_))
                    ssum, B_ssum = ssum_ring.next()
                    onb, onbb = onb_ring.next()

                    def post1(res=res, ssum=ssum, B_ssum=B_ssum):
                        (o0, ob0), (o1, ob1) = res
                        stt(DVE, o0, o1, lamc[:, 1:2], o0, ALU.mult, ALU.add, [ob1, ob0, B_lamc], [ob0])
                        tt(DVE, osq, o0, o0, ALU.mult, [ob0], [B_osq])
                        P.op(DVE, lambda e: e.reduce_sum(out=ssum[:, 0:4], in_=osq, axis=AX.X), [B_osq], [B_ssum])

                    def post2(ssum=ssum, B_ssum=B_ssum):
                        act(ssum[:, 4:8], ssum[:, 0:4], AF.Ln, [B_ssum, B_epsc], [B_ssum], bias=epsc[:, 0:1], scale=float(1.0 / 128.0))
                        act(ssum[:, 8:12], ssum[:, 4:8], AF.Exp, [B_ssum], [B_ssum], scale=-0.5)

                    def post3(res=res, ssum=ssum, B_ssum=B_ssum, onb=onb, onbb=onbb):
                        (o0, ob0), (o1, ob1) = res
                        tt(DVE, o0, o0, ssum[:, 8:12].unsqueeze(2).to_broadcast([128, 4, 128]), ALU.mult, [ob0, B_ssum], [ob0])
                        tt(DVE, onb, o0, gsub[:, :].unsqueeze(1).to_broadcast([128, 4, 128]), ALU.mult, [ob0, B_gsub], [onbb])
                    defer(ST_EARLY[0], post1)
                    defer(ST_EARLY[1], post2)
                    defer(ST_EARLY[2], post3)
                    defer(ST_FIN, lambda TB, jb=jb, h=h, i=i, onb=onb, onbb=onbb: finish(jb, h, jb.OAT, i, onb, onbb, jb.ostores, TB))
            else:
                for i in range(4):
                    ab, accv = attention(jb, Kv, Kb, Vv, Vb, Qv[:, 0, :], Qb, i, (0, 128, math.sqrt(128.0)), 8, None)
                    onb, onbb = onb_ring.next()
                    asrc, ards = acc_out(ab, accv)

                    def postb(asrc=asrc, ards=ards, onb=onb, onbb=onbb):
                        rz, rzb = rz_ring.next()
                        P.op(DVE, lambda e, rz=rz, asrc=asrc: e.reciprocal(out=rz, in_=asrc[:, :, 128:129]), ards, [rzb])
                        tt(DVE, onb, asrc[:, :, 0:128], rz.to_broadcast([128, 4, 128]), ALU.mult, ards + [rzb], [onbb])
                    defer(ST_EARLY[0], postb)
                    defer(ST_FIN, lambda TB, jb=jb, h=h, i=i, onb=onb, onbb=onbb: finish(jb, h, jb.OBT, i, onb, onbb, jb.ostores, TB))
        for k_ in ST_EARLY:
            run_stage(k_)
        run_stage(ST_FIN, 0 if PAIR else 3)
        P.barrier()

    ystores = []
    if "C" in phases:
        AC = Alloc(SMALL_END)
        lnb = AC([128, 4, D], F32)
        B_lnb = Buf("lnb")
        dma(SP, lnb, lnp.partition_broadcast(128), [], [B_lnb], B_lnb)
        cxb_ring = Ring(2, lambda i: (AC([128, KC, NT], BF16), Buf("cxb%d" % i)))
        oa_ring = Ring(1, lambda i: (AC([128, KC, NT], BF16), Buf("coa%d" % i)))
        ob_ring = Ring(1, lambda i: (AC([128, KC, NT], BF16), Buf("cob%d" % i)))
        xtok_ring = Ring(1, lambda i: (AC([128, 4, D], F32), Buf("cxt%d" % i)))
        sga = AC([128, 4, NT], F32)
        B_sga = Buf("sga")
        sgb = AC([128, 4, NT], F32)
        B_sgb = Buf("sgb")
        mT = AC([128, KC, NT], BF16)
        B_mT = Buf("mT")
        htok = AC([128, 4, D], F32)
        B_htok = [Buf("htok%d" % i) for i in range(4)]
        hb_ring = Ring(2, lambda i: (AC([128, 2, D], BF16), Buf("hb%d" % i)))
        hT = AC([128, KC, NT], BF16)
        B_hT = Buf("hT")
        hidT = AC([128, NFC, NT], BF16)
        B_hid = Buf("hidT")
        sil_ring = Ring(2, lambda i: (AC([128, NT], F32), Buf("sil%d" % i)))
        junk = AC([128, D], BF16)
        B_junk = Buf("junk")
        stat = AC([128, 2, 16], F32)
        B_stat = [Buf("stat0"), Buf("stat1")]
        cw_ring = Ring(4, lambda i: (AC([128, KC, NT], BF16), Buf("cw%d" % i)))
        fm_ps = Ring(4, lambda i: i)
        ps_bf0 = [ps[:, b, :].bitcast(BF16) for b in range(4)]

        tile_specs = []
        for og in range(2):
            tile_specs.append((wb_in[OFF_GA // NT + og], NT, KC, B_w["in"]))
            tile_specs.append((wb_in[OFF_GB // NT + og], NT, KC, B_w["in"]))
            tile_specs.append((wb_pa[og], NT, KC, B_w["pa"]))
            tile_specs.append((wb_pb[og], NT, KC, B_w["pb"]))
        for half in range(2):
            tile_specs.append((wb_o[half], NT, KC, B_w["o"]))
        for fg in range(6):
            ncol = min(512, DFF - fg * 512)
            tile_specs.append((wb_g[fg][:, :, 0:ncol], ncol, KC, B_w["g"]))
            tile_specs.append((wb_u[fg][:, :, 0:ncol], ncol, KC, B_w["u"]))
        for half in range(2):
            for (fc0, nfc) in ((0, 8), (8, 8), (16, 6)):
                tile_specs.append((wb_d[half][:, fc0:fc0 + nfc, :], NT, nfc, B_w["d"]))
        ctiles = [(jb, i) for jb in jobs for i in range(4)]
        sp_merge, sp_wo, sp_ffn = tile_specs[0:8], tile_specs[8:10], tile_specs[10:]
        rot_specs = sp_merge + sp_wo
        for ti_ in range(len(ctiles)):
            if ti_ + 1 < len(ctiles):
                rot_specs = rot_specs + sp_merge
            rot_specs = rot_specs + sp_ffn
            if ti_ + 1 < len(ctiles):
                rot_specs = rot_specs + sp_wo
        wsC = Stream(cw_ring, rot_specs,
                     lambda v, b, sp: dma(SP, v[:, 0:sp[2], 0:sp[1]], sp[0], [sp[3]], [b], b), 2)
        xsC = Stream(cxb_ring, ctiles, lambda v, b, sp: dma(POOL, v, xT[sp[0].xt0 + sp[1]], [], [b], b), 1)
        oaS = Stream(oa_ring, ctiles, lambda v, b, sp: dma(SP, v, sp[0].OAT[:, :, sp[1] * NT:(sp[1] + 1) * NT].rearrange("h e t -> e h t"),
                                                       [], [b], b, extra=sp[0].ostores), 0)
        obS = Stream(ob_ring, ctiles, lambda v, b, sp: dma(SP, v, sp[0].OBT[:, :, sp[1] * NT:(sp[1] + 1) * NT].rearrange("h e t -> e h t"),
                                                       [], [b], b, extra=sp[0].ostores), 0)
        xtS = Stream(xtok_ring, ctiles, lambda v, b, sp: dma(SP, v, xtok[sp[0].qt0 + sp[1]], [], [b], b), 0)

        bg = []

        def tick(n=1):
            if os.environ.get('NOTICK'):
                return
            for _ in range(n):
                if bg:
                    bg.pop(0)()

        def flush():
            while bg:
                bg.pop(0)()

        def layer_norm(gi, cont):
            def hv(ch):
                return htok[:, 2 * ch:2 * ch + 2, :]

            def hb_(ch):
                return [B_htok[2 * ch], B_htok[2 * ch + 1]]

            def st1(ch):
                P.op(DVE, lambda e, ch=ch: e.reduce_sum(out=stat[:, ch, 0:2], in_=hv(ch), axis=AX.X), hb_(ch), [B_stat[ch]])
                ts(DVE, stat[:, ch, 2:4], stat[:, ch, 0:2], float(-1.0 / D), None, ALU.mult, None, [B_stat[ch]], [B_stat[ch]])
                P.op(DVE, lambda e, ch=ch: e.memset(stat[:, ch, 4:6], 0.0), [], [B_stat[ch]])

            def st2(ch):
                tt(DVE, hv(ch), hv(ch), stat[:, ch, 2:4].unsqueeze(2).to_broadcast([128, 2, D]), ALU.add,
                   hb_(ch) + [B_stat[ch]], hb_(ch))

            def st3(ch):
                for s2 in range(2):
                    act(junk, htok[:, 2 * ch + s2, :], AF.Square, [B_htok[2 * ch + s2]], [B_junk, B_stat[ch]],
                        accum=stat[:, ch, 4 + s2:5 + s2])
                act(stat[:, ch, 6:8], stat[:, ch, 4:6], AF.Ln, [B_stat[ch], B_epsc], [B_stat[ch]], bias=epsc[:, 1:2], scale=float(1.0 / D))
                act(stat[:, ch, 8:10], stat[:, ch, 6:8], AF.Exp, [B_stat[ch]], [B_stat[ch]], scale=-0.5)

            def st4(ch):
                for s2 in range(2):
                    sub = 2 * ch + s2
                    stt(DVE, htok[:, sub, :], htok[:, sub, :], stat[:, ch, 8 + s2:9 + s2], lnb[:, gi, :], ALU.mult, ALU.mult,
                        [B_htok[sub], B_stat[ch], B_lnb], [B_htok[sub]])

            def st5(ch):
                tt(POOL, hv(ch), hv(ch), lnb[:, gi + 1, :].unsqueeze(1).to_broadcast([128, 2, D]), ALU.add,
                   hb_(ch) + [B_lnb], hb_(ch))
                cont(ch)
            for st in (st1, st2, st3, st4, st5):
                for ch in range(2):
                    bg.append(lambda st=st, ch=ch: st(ch))

        def merge(jb, i):
            xv, xbuf = xsC.get()
            oav, oabuf = oaS.get()
            obv, obbuf = obS.get()
            for og in range(2):
                for (gate_dst, gbuf) in ((sga, B_sga), (sgb, B_sgb)):
                    wv, wbuf = wsC.get()
                    for oc in range(4):
                        bk = fm_ps.next()
                        for kc in range(KC):
                            mm(ps[:, bk, :], wv[:, kc, oc * 128:(oc + 1) * 128], xv[:, kc, :], kc == 0, kc == KC - 1,
                               [wbuf, xbuf], [psb[bk]])
                        act(gate_dst[:, oc, :], ps[:, bk, :], AF.Sigmoid, [psb[bk]], [gbuf])
                        tick()
                wv, wbuf = wsC.get()
                for oc in range(4):
                    bk = fm_ps.next()
                    for kc in range(KC):
                        mm(ps[:, bk, :], wv[:, kc, oc * 128:(oc + 1) * 128], oav[:, kc, :], kc == 0, kc == KC - 1,
                           [wbuf, oabuf], [psb[bk]])
                    tt(DVE, sga[:, oc, :], ps[:, bk, :], sga[:, oc, :], ALU.mult, [psb[bk], B_sga], [B_sga])
                    tick()
                wv, wbuf = wsC.get()
                for oc in range(4):
                    bk = fm_ps.next()
                    for kc in range(KC):
                        mm(ps[:, bk, :], wv[:, kc, oc * 128:(oc + 1) * 128], obv[:, kc, :], kc == 0, kc == KC - 1,
                           [wbuf, obbuf], [psb[bk]])
                    tt(DVE, sgb[:, oc, :], ps[:, bk, :], sgb[:, oc, :], ALU.mult, [psb[bk], B_sgb], [B_sgb])
                    tt(POOL, mT[:, og * 4 + oc, :], sga[:, oc, :], sgb[:, oc, :], ALU.add, [B_sga, B_sgb], [B_mT])
                    tick()
            oaS.prefetch()
            obS.prefetch()

        def wo_ln1(jb, i):
            xtv, xtbuf = xtS.get()
            for half in range(2):
                wv, wbuf = wsC.get()
                for sub in range(4):
                    bk = 4 + sub
                    for kc in range(KC):
                        mm(ps[:, bk, :], mT[:, kc, sub * 128:(sub + 1) * 128], wv[:, kc, :], kc == 0, kc == KC - 1,
                           [B_mT, wbuf], [psb[bk]])
                    tick(3)
                flush()
                for sub in range(4):
                    bk = 4 + sub
                    stt(DVE, htok[:, sub, half * 512:(half + 1) * 512], xtv[:, sub, half * 512:(half + 1) * 512], float(ALPHA),
                        ps[:, bk, :], ALU.mult, ALU.add, [xtbuf, psb[bk]], [B_htok[sub]])
            xtS.prefetch()

            def after_ln1(ch):
                hbv, hbbuf = hb_ring.next()
                cp(ACT, hbv, htok[:, 2 * ch:2 * ch + 2, :], [B_htok[2 * ch], B_htok[2 * ch + 1]], [hbbuf])
                for s2 in range(2):
                    sub = 2 * ch + s2
                    bk = fm_ps.next()
                    for kc in range(KC):
                        tr(ps_bf0[bk][:, kc * 128:(kc + 1) * 128], hbv[:, s2, kc * 128:(kc + 1) * 128], ident, [hbbuf, B_cm], [psb[bk]])
                    cp(DVE, hT[:, :, sub * 128:(sub + 1) * 128], ps_bf0[bk][:, 0:1024].rearrange("p (k t) -> p k t", k=KC),
                       [psb[bk]], [B_hT])
            layer_norm(0, after_ln1)

        def ffn(jb, i):
            qt = jb.qt0 + i
            for fg in range(6):
                ncol = min(512, DFF - fg * 512)
                wgv, wgbuf = wsC.get()
                wuv, wubuf = wsC.get()
                for oc in range(ncol // 128):
                    fc = fg * 4 + oc
                    bg_ = fm_ps.next()
                    for kc in range(KC):
                        mm(ps[:, bg_, :], wgv[:, kc, oc * 128:(oc + 1) * 128], hT[:, kc, :], kc == 0, kc == KC - 1,
                           [wgbuf, B_hT], [psb[bg_]])
                    bu = fm_ps.next()
                    for kc in range(KC):
                        mm(ps[:, bu, :], wuv[:, kc, oc * 128:(oc + 1) * 128], hT[:, kc, :], kc == 0, kc == KC - 1,
                           [wubuf, B_hT], [psb[bu]])
                    sv, sbuf_ = sil_ring.next()
                    act(sv, ps[:, bg_, :], AF.Silu, [psb[bg_]], [sbuf_])
                    tt(DVE, hidT[:, fc, :], ps[:, bu, :], sv, ALU.mult, [psb[bu], sbuf_], [B_hid])
            for half in range(2):
                for (fc0, nfc) in ((0, 8), (8, 8), (16, 6)):
                    wv, wbuf = wsC.get()
                    for sub in range(4):
                        bk = 4 + sub
                        for j in range(nfc):
                            fc = fc0 + j
                            mm(ps[:, bk, :], hidT[:, fc, sub * 128:(sub + 1) * 128], wv[:, j, :], fc == 0, fc == NFC - 1,
                               [B_hid, wbuf], [psb[bk]])
                for sub in range(4):
                    bk = 4 + sub
                    stt(DVE, htok[:, sub, half * 512:(half + 1) * 512], htok[:, sub, half * 512:(half + 1) * 512], float(ALPHA),
                        ps[:, bk, :], ALU.mult, ALU.add, [B_htok[sub], psb[bk]], [B_htok[sub]])

            def after_ln2(ch, qt=qt):
                for s2 in range(2):
                    sub = 2 * ch + s2
                    o = dma(SP, y[qt, :, sub, :], htok[:, sub, :], [B_htok[sub]], [], B_htok[sub])
                    ystores.append(o)
            layer_norm(2, after_ln2)

        nct = len(ctiles)
        merge(*ctiles[0])
        wo_ln1(*ctiles[0])
        for ti in range(nct):
            if ti + 1 < nct:
                merge(*ctiles[ti + 1])
            flush()
            ffn(*ctiles[ti])
            if ti + 1 < nct:
                wo_ln1(*ctiles[ti + 1])
        flush()

    P.finalize()
    sems = {e: [es.enter_context(nc.semaphore("sem_%s_%d" % (e, k))) for k in range(P.n_epochs(e))] for e in ENGS}
    dsems = {}
    for b in P.dmabufs:
        dsems[id(b)] = es.enter_context(nc.semaphore("d_" + b.name))
    blk = es.enter_context(nc.Block())

    @blk.sync
    def _(e):
        P.emit_engine(SP, e, sems, dsems)
        for b in P.dmabufs:
            e.wait_ge(dsems[id(b)], 16 * b.ndma)

    @blk.tensor
    def _(e):
        P.emit_engine(PE, e, sems, dsems)

    @blk.scalar
    def _(e):
        P.emit_engine(ACT, e, sems, dsems)

    @blk.vector
    def _(e):
        P.emit_engine(DVE, e, sems, dsems)

    @blk.gpsimd
    def _(e):
        P.emit_engine(POOL, e, sems, dsems)

    es.close()
    nops = {e: len(P.ops[e]) for e in ENGS}
    return nc, nops, jobs


def _tiles_T(x):
    n = x.shape[0] // NT
    return np.ascontiguousarray(x.reshape(n, NT, KC, 128).transpose(0, 3, 2, 1))


def _tiles_tok(x):
    n = x.shape[0] // NT
    return np.ascontiguousarray(x.reshape(n, 4, 128, D).transpose(0, 2, 1, 3))


def _rope_tiles(pos):
    cos, sin = _rope_tables(pos)
    n = pos.shape[0] // NT
    c = cos.reshape(n, NT, 128).transpose(0, 2, 1)
    s = sin.reshape(n, NT, 128).transpose(0, 2, 1)
    return np.ascontiguousarray(np.stack([c, s], axis=2))


def _const_mats():
    cm = np.zeros((128, 4, 128), np.float32)
    cm[np.arange(128), 0, np.arange(128)] = 1.0
    cm[np.arange(128), 1, 127 - np.arange(128)] = 1.0
    for dp in range(128):
        if (dp % 64) < 32:
            cm[dp + 32, 2, dp] = -1.0
        else:
            cm[dp - 32, 2, dp] = 1.0
    cm[:, 3, :] = 1.0
    return cm


def _core_inputs(c, inp, n_sample, with_prompt):
    b, j = c // 4, c % 4
    xs = np.asarray(inp["x_sample"], np.float32)
    xp = np.asarray(inp["x_prompt"], np.float32)
    xT_l, xtok_l = [], []
    for s in range(n_sample):
        seq = xs[4 * c + s]
        xT_l.append(_tiles_T(seq))
        xtok_l.append(_tiles_tok(seq))
    rope_l = [_rope_tiles(np.arange(2048))]
    vf_l = [np.ones((4, 128, 4), np.float32)]
    sel_l = [_fb_sel([128 * s for s in range(16)], 0, 16).reshape(33, 64)]
    if with_prompt:
        slots = _prompt_slots(j)
        pos = np.concatenate([(np.arange(128) + k0) if k0 is not None else -np.ones(128, np.int64) for k0 in slots])
        xg = xp[b][np.maximum(pos, 0)] * (pos >= 0)[:, None].astype(np.float32)
        xT_l.append(_tiles_T(xg))
        xtok_l.append(_tiles_tok(xp[b][2048 * j:2048 * (j + 1)]))
        rope_l.append(_rope_tiles(pos))
        vf_l.append((pos >= 0).astype(np.float32).reshape(17, 4, 128).transpose(0, 2, 1))
        sel_l.append(_fb_sel(slots[:NSLOT_P], 2048 * j, NSLOT_P).reshape(33, 4 * NSLOT_P))
    t = np.arange(1280)
    bk = _t5_bucket(639 - t)
    oh4 = np.zeros((32, 1280), np.float32)
    oh4[bk, t] = 1.0
    f = lambda k: np.ascontiguousarray(np.asarray(inp[k], np.float32)[0])
    m = {
        "xT": np.concatenate(xT_l, 0), "xtok": np.concatenate(xtok_l, 0), "ropeT": np.concatenate(rope_l, 0), "vflag": np.ascontiguousarray(np.concatenate(vf_l, 0)),
        "w_in": f("w_in"), "w_pa": f("w_proj_a"), "w_pb": f("w_proj_b"), "w_o": f("w_o"),
        "w_g": f("w_gate"), "w_u": f("w_up"), "w_d": f("w_down"),
        "lam4": np.stack([f("lambda_q1"), f("lambda_k1"), f("lambda_q2"), f("lambda_k2")]),
        "subln_g": np.asarray(inp["subln_g"], np.float32), "qn_g": np.asarray(inp["q_norm_g"], np.float32),
        "kn_g": np.asarray(inp["k_norm_g"], np.float32), "rel_bias": np.asarray(inp["rel_bias"], np.float32),
        "lnp": np.stack([f("ln1_g"), f("ln1_b"), f("ln2_g"), f("ln2_b")]),
        "oh4": oh4, "selT": np.ascontiguousarray(np.concatenate(sel_l, 1)), "cmat": _const_mats(),
    }
    return m


_CACHE = {}


def kernel(**inputs):
    if "nc" not in _CACHE:
        _CACHE["nc"] = build_program(4, True, "0ABC")[0]
    nc = _CACHE["nc"]
    in_maps = [_core_inputs(c, inputs, 4, True) for c in range(8)]
    res = run_bass_kernel_spmd(nc, in_maps, core_ids=list(range(8)))
    y_p = np.zeros((2, 8192, D), np.float32)
    y_s = np.zeros((32, 2048, D), np.float32)
    for c in range(8):
        yc = np.asarray(res.results[c]["y"])
        yc = yc.transpose(0, 2, 1, 3).reshape(20, NT, D)
        for s in range(4):
            y_s[4 * c + s] = yc[4 * s:4 * s + 4].reshape(2048, D)
        b, j = c // 4, c % 4
        y_p[b, 2048 * j:2048 * (j + 1)] = yc[16:20].reshape(2048, D)
    return (y_p, y_s)
```
